# Optimizing a Trainium2 kernel written in Bass

```python
import math
import jax, jax.numpy as jnp
from jax import lax
import numpy as np

D_MODEL = 2048
BATCH = 4
SEQ = 8192
DEPTH = 4
DEC_BATCH = 1
DEC_SEQ = 16384
PAST_LEN = 128

N_MIXERS = 2
N_HYENA = (DEPTH + 1) // 2
N_ATTN = DEPTH // 2
HY_ORDER = 2
HY_SHORT = 3
HY_EMB = 33
HY_BANDS = (HY_EMB - 1) // 2
HY_FILTER_W = 64
HY_FAST_PCT = 0.3
HY_SLOW_PCT = 1.5
HY_TARGET = 1e-2
N_HEADS = 16
HEAD_DIM = 128
N_KV = 4
GQA_G = N_HEADS // N_KV
WINDOW = 128
BLOCK = 128
N_BUCKETS = 32
MAX_DIST = 128
D_FF = -(-(8 * D_MODEL) // (3 * 256)) * 256
EPS = 1e-6
NEG = -1e30

kernel_name = "hyena_swa_gqa_hybrid_encoder"


def rmsnorm(x, g):
    xf = x.astype(jnp.float32)
    y = xf * lax.rsqrt(jnp.mean(xf * xf, axis=-1, keepdims=True) + EPS)
    return (y * g.astype(jnp.float32)).astype(x.dtype)


def swiglu(h, w_gate_up, w_down):
    gu = h @ w_gate_up
    g, u = gu[..., :D_FF], gu[..., D_FF:]
    return (jax.nn.silu(g) * u) @ w_down


def hyena_filters(L, f_w1, f_b1, f_w2, f_b2, f_w3, f_b3, f_wout, f_freq):
    t = jnp.linspace(0.0, 1.0, L, dtype=jnp.float32)[:, None]
    w = 2.0 * math.pi * jnp.arange(L, dtype=jnp.float32) / L
    f = jnp.linspace(1e-4, HY_BANDS - 1, HY_BANDS, dtype=jnp.float32)
    ang = w[:, None] * f[None, :]
    feats = jnp.concatenate([t, jnp.cos(ang), -jnp.sin(ang)], axis=-1)
    fr = f_freq.astype(jnp.float32)
    a = jnp.sin(fr * (feats @ f_w1 + f_b1))
    a = jnp.sin(fr * (a @ f_w2 + f_b2))
    a = jnp.sin(fr * (a @ f_w3 + f_b3))
    h = (a @ f_wout).astype(jnp.float32).reshape(L, HY_ORDER, 2, D_MODEL)
    deltas = np.abs(np.linspace(math.log(HY_TARGET) / HY_SLOW_PCT,
                                math.log(HY_TARGET) / HY_FAST_PCT, D_MODEL)).astype(np.float32)
    decay = jnp.exp(-t * deltas[None, :])
    h = h * decay[:, None, None, :]
    hf = h[:, :, 0]
    hb = h[:, :, 1]
    k = jnp.concatenate([hf, jnp.zeros((1, HY_ORDER, D_MODEL), jnp.float32), hb[:0:-1]], axis=0)
    k = k / jnp.sum(jnp.abs(k), axis=0, keepdims=True)
    return jnp.fft.rfft(k, n=2 * L, axis=0)


def long_conv(z, kf):
    L = z.shape[1]
    Z = jnp.fft.rfft(z.astype(jnp.float32), n=2 * L, axis=1)
    y = jnp.fft.irfft(Z * kf[None], n=2 * L, axis=1)[:, :L]
    return y.astype(z.dtype)


def hyena_mixer(h, w_in, b_in, conv_w, conv_b, f_w1, f_b1, f_w2, f_b2, f_w3, f_b3,
                f_wout, f_freq, skip, w_out, b_out):
    B, L, _ = h.shape
    u = h @ w_in + b_in
    up = jnp.pad(u, ((0, 0), (1, 1), (0, 0)))
    uc = conv_w[0] * up[:, :-2] + conv_w[1] * up[:, 1:-1] + conv_w[2] * up[:, 2:] + conv_b
    v, x1, x2 = uc[..., :D_MODEL], uc[..., D_MODEL:2 * D_MODEL], uc[..., 2 * D_MODEL:]
    kf = hyena_filters(L, f_w1, f_b1, f_w2, f_b2, f_w3, f_b3, f_wout, f_freq)
    z = v
    for o, gate in enumerate((x1, x2)):
        z = gate * (long_conv(z, kf[:, o]) + skip[o] * z)
    return z @ w_out + b_out


def _band_structure():
    qi = np.arange(BLOCK)[:, None]
    ki = np.arange(3 * BLOCK)[None, :]
    rel = ki - BLOCK - qi
    nb = N_BUCKETS // 2
    max_exact = nb // 2
    n = np.abs(rel)
    large = max_exact + (np.log(np.maximum(n, 1) / max_exact) / math.log(MAX_DIST / max_exact)
                         * (nb - max_exact)).astype(np.int32)
    large = np.minimum(large, nb - 1)
    buckets = (rel > 0).astype(np.int32) * nb + np.where(n < max_exact, n, large).astype(np.int32)
    band = n <= WINDOW
    return buckets, band


def window_attention(h, w_qkv, q_g, k_g, sink, w_o, rel_bias):
    B, L, _ = h.shape
    nb = L // BLOCK
    qkv = h @ w_qkv
    nq, nk = N_HEADS * HEAD_DIM, N_KV * HEAD_DIM
    q = rmsnorm(qkv[..., :nq].reshape(B, L, N_HEADS, HEAD_DIM), q_g)
    k = rmsnorm(qkv[..., nq:nq + nk].reshape(B, L, N_KV, HEAD_DIM), k_g)
    v = qkv[..., nq + nk:].reshape(B, L, N_KV, HEAD_DIM)
    qb = q.reshape(B, nb, BLOCK, N_KV, GQA_G, HEAD_DIM)

    def windows(t):
        tp = jnp.pad(t, ((0, 0), (BLOCK, BLOCK), (0, 0), (0, 0))).reshape(B, nb + 2, BLOCK, N_KV, HEAD_DIM)
        return jnp.concatenate([tp[:, :-2], tp[:, 1:-1], tp[:, 2:]], axis=2)

    kw, vw = windows(k), windows(v)
    s = jnp.einsum("bnqhgd,bnkhd->bnhgqk", qb, kw).astype(jnp.float32) * (HEAD_DIM ** -0.5)
    buckets, band = _band_structure()
    bias = rel_bias[buckets].astype(jnp.float32)
    bias = jnp.transpose(bias, (2, 0, 1)).reshape(N_KV, GQA_G, BLOCK, 3 * BLOCK)
    kpos = np.arange(nb)[:, None] * BLOCK + np.arange(3 * BLOCK)[None, :] - BLOCK
    valid = band[None] & ((kpos >= 0) & (kpos < L))[:, None, :]
    s = jnp.where(valid[None, :, None, None], s + bias, NEG)
    sk = sink.astype(jnp.float32).reshape(1, 1, N_KV, GQA_G, 1, 1)
    m = jnp.maximum(jnp.max(s, axis=-1, keepdims=True), sk)
    p = jnp.exp(s - m)
    p = p / (jnp.sum(p, axis=-1, keepdims=True) + jnp.exp(sk - m))
    o = jnp.einsum("bnhgqk,bnkhd->bnqhgd", p.astype(vw.dtype), vw).reshape(B, L, nq)
    return o @ w_o


def trunk(x, norm_mix_g, norm_ffn_g, hy_w_in, hy_b_in, hy_conv_w, hy_conv_b,
          hy_f_w1, hy_f_b1, hy_f_w2, hy_f_b2, hy_f_w3, hy_f_b3, hy_f_wout, hy_f_freq,
          hy_skip, hy_w_out, hy_b_out, at_w_qkv, at_q_g, at_k_g, at_sink, at_w_o,
          rel_bias, ffn_w_gate_up, ffn_w_down):
    for i in range(DEPTH):
        j = i // N_MIXERS
        h = rmsnorm(x, norm_mix_g[i])
        if i % N_MIXERS == 0:
            y = hyena_mixer(h, hy_w_in[j], hy_b_in[j], hy_conv_w[j], hy_conv_b[j],
                            hy_f_w1[j], hy_f_b1[j], hy_f_w2[j], hy_f_b2[j], hy_f_w3[j], hy_f_b3[j],
                            hy_f_wout[j], hy_f_freq[j], hy_skip[j], hy_w_out[j], hy_b_out[j])
        else:
            y = window_attention(h, at_w_qkv[j], at_q_g[j], at_k_g[j], at_sink[j], at_w_o[j], rel_bias)
        x = x + y
        x = x + swiglu(rmsnorm(x, norm_ffn_g[i]), ffn_w_gate_up[i], ffn_w_down[i])
    return x


def setup_inputs(seed: int = 0) -> dict:
    key = jax.random.key(seed)
    ks = jax.random.split(key, 32)

    def nrm(k, shape, scale):
        return jax.random.normal(k, shape, jnp.float32) * scale

    D = D_MODEL
    qkv_w = (N_HEADS + 2 * N_KV) * HEAD_DIM
    return {
        "x_prompt": nrm(ks[0], (BATCH, SEQ, D), 1.0),
        "x_sample": nrm(ks[1], (DEC_BATCH, DEC_SEQ, D), 1.0),
        "norm_mix_g": 1.0 + nrm(ks[2], (DEPTH, D), 0.02),
        "norm_ffn_g": 1.0 + nrm(ks[3], (DEPTH, D), 0.02),
        "hy_w_in": nrm(ks[4], (N_HYENA, D, 3 * D), D ** -0.5),
        "hy_b_in": nrm(ks[5], (N_HYENA, 3 * D), 0.02),
        "hy_conv_w": nrm(ks[6], (N_HYENA, HY_SHORT, 3 * D), HY_SHORT ** -0.5),
        "hy_conv_b": nrm(ks[7], (N_HYENA, 3 * D), 0.02),
        "hy_f_w1": nrm(ks[8], (N_HYENA, HY_EMB, HY_FILTER_W), HY_EMB ** -0.5),
        "hy_f_b1": nrm(ks[9], (N_HYENA, HY_FILTER_W), 0.02),
        "hy_f_w2": nrm(ks[10], (N_HYENA, HY_FILTER_W, HY_FILTER_W), HY_FILTER_W ** -0.5),
        "hy_f_b2": nrm(ks[11], (N_HYENA, HY_FILTER_W), 0.02),
        "hy_f_w3": nrm(ks[12], (N_HYENA, HY_FILTER_W, HY_FILTER_W), HY_FILTER_W ** -0.5),
        "hy_f_b3": nrm(ks[13], (N_HYENA, HY_FILTER_W), 0.02),
        "hy_f_wout": nrm(ks[14], (N_HYENA, HY_FILTER_W, HY_ORDER * 2 * D), HY_FILTER_W ** -0.5),
        "hy_f_freq": 1.0 + nrm(ks[15], (N_HYENA, HY_FILTER_W), 0.1),
        "hy_skip": nrm(ks[16], (N_HYENA, HY_ORDER, D), 1.0),
        "hy_w_out": nrm(ks[17], (N_HYENA, D, D), D ** -0.5),
        "hy_b_out": nrm(ks[18], (N_HYENA, D), 0.02),
        "at_w_qkv": nrm(ks[19], (N_ATTN, D, qkv_w), D ** -0.5),
        "at_q_g": 1.0 + nrm(ks[20], (N_ATTN, HEAD_DIM), 0.02),
        "at_k_g": 1.0 + nrm(ks[21], (N_ATTN, HEAD_DIM), 0.02),
        "at_sink": nrm(ks[22], (N_ATTN, N_HEADS), 1.0),
        "at_w_o": nrm(ks[23], (N_ATTN, N_HEADS * HEAD_DIM, D), (N_HEADS * HEAD_DIM) ** -0.5),
        "rel_bias": nrm(ks[24], (N_BUCKETS, N_HEADS), 0.1),
        "ffn_w_gate_up": nrm(ks[25], (DEPTH, D, 2 * D_FF), D ** -0.5),
        "ffn_w_down": nrm(ks[26], (DEPTH, D_FF, D), D_FF ** -0.5),
    }


def reference(x_prompt, x_sample, norm_mix_g, norm_ffn_g, hy_w_in, hy_b_in, hy_conv_w, hy_conv_b,
              hy_f_w1, hy_f_b1, hy_f_w2, hy_f_b2, hy_f_w3, hy_f_b3, hy_f_wout, hy_f_freq,
              hy_skip, hy_w_out, hy_b_out, at_w_qkv, at_q_g, at_k_g, at_sink, at_w_o,
              rel_bias, ffn_w_gate_up, ffn_w_down):
    y_prompt = trunk(x_prompt, norm_mix_g, norm_ffn_g, hy_w_in, hy_b_in, hy_conv_w, hy_conv_b,
                     hy_f_w1, hy_f_b1, hy_f_w2, hy_f_b2, hy_f_w3, hy_f_b3, hy_f_wout, hy_f_freq,
                     hy_skip, hy_w_out, hy_b_out, at_w_qkv, at_q_g, at_k_g, at_sink, at_w_o,
                     rel_bias, ffn_w_gate_up, ffn_w_down)
    y_sample = trunk(x_sample, norm_mix_g, norm_ffn_g, hy_w_in, hy_b_in, hy_conv_w, hy_conv_b,
                     hy_f_w1, hy_f_b1, hy_f_w2, hy_f_b2, hy_f_w3, hy_f_b3, hy_f_wout, hy_f_freq,
                     hy_skip, hy_w_out, hy_b_out, at_w_qkv, at_q_g, at_k_g, at_sink, at_w_o,
                     rel_bias, ffn_w_gate_up, ffn_w_down)
    return (y_prompt, y_sample)
```

```python
import math
from contextlib import ExitStack

import numpy as np
import concourse.bass as bass
import concourse.mybir as mybir
from concourse.bass_utils import run_bass_kernel_spmd

F32 = mybir.dt.float32
BF16 = mybir.dt.bfloat16
AF = mybir.ActivationFunctionType
ALU = mybir.AluOpType
AX = mybir.AxisListType

T = 16384
TT = 512
NTT = T // TT
HD = 128
NSLOT = 32768
SEG = 512
NSEG = NSLOT // SEG
EPS = 1e-6
NEGM = -30000.0

ENGS = ("pe", "act", "dve", "pool", "sp")
NDMASEM = 4
SELF_SYNC = True


_ALLBUFS = []


class Buf:
    __slots__ = ("name", "w", "r")

    def __init__(self, name):
        self.name = name
        self.w = None
        self.r = []
        _ALLBUFS.append(self)


class Prog:
    def __init__(self, nc, sems):
        self.nc = nc
        self.sems = sems
        self.ins = []
        self.cnt = {e: 0 for e in ENGS}
        self.dcount = {}
        self.dma_n = {e: 0 for e in ENGS}
        self.known = {e: {} for e in ENGS}
        self.last_tok = {}
        self.nins = 0
        self.nwaits = 0
        self.persist = []
        self.only = None
        self.cur_on = True
        self.first = True

    def op(self, eng, fn, reads=(), writes=(), dma=False):
        idx = len(self.ins)
        deps = []
        for b in reads:
            if b.w is not None:
                deps.append(b.w)
        for b in writes:
            if b.w is not None:
                deps.append(b.w)
            deps.extend(b.r)
        self.ins.append(dict(idx=idx, eng=eng, fn=fn, dma=dma, deps=deps, sig=False))
        for b in reads:
            b.r.append(idx)
        for b in writes:
            b.w = idx
            b.r = []
        return idx

    def dma(self, eng, out, in_, reads=(), writes=()):
        return self.op(eng, lambda e: e.dma_start(out=out, in_=in_), reads, writes, dma=True)

    def phase(self, name):
        self.cur_on = (self.only is None) or (name in self.only)

    def flush(self):
        if not self.cur_on and not self.first:
            self.ins = []
            for b in _ALLBUFS:
                b.w = None
                b.r = []
            del _ALLBUFS[:]
            for t_, b in self.persist:
                _ALLBUFS.append(b)
            return
        self.first = False
        nc = self.nc
        ins = self.ins
        sems = self.sems
        per_eng = {e: [] for e in ENGS}
        for rec in ins:
            per_eng[rec["eng"]].append(rec)

        def need_sync(p, rec):
            if p["dma"] or rec["dma"]:
                return True
            if p["eng"] == rec["eng"] and p["eng"] not in self.self_sync:
                return False
            return True

        for rec in ins:
            for d in rec["deps"]:
                p = ins[d]
                if not p["dma"] and need_sync(p, rec):
                    p["sig"] = True
        for e in ENGS:
            for rec in reversed(per_eng[e]):
                if not rec["dma"]:
                    rec["sig"] = True
                    break
        for rec in ins:
            e = rec["eng"]
            if rec["dma"]:
                k = self.dma_n[e] % NDMASEM
                self.dma_n[e] += 1
                key = ("dma", e, k)
                prev = self.dcount.get(key, 0)
                rec["prev_dma"] = (key, prev)
                self.dcount[key] = prev + 16
                rec["tok"] = (key, prev + 16)
            elif rec["sig"]:
                self.cnt[e] += 1
                rec["tok"] = (e, self.cnt[e])
        engobj = {"pe": nc.tensor, "act": nc.scalar, "dve": nc.vector, "pool": nc.gpsimd, "sp": nc.sync}
        for e in ENGS:
            eo = engobj[e]
            known = self.known[e]
            for rec in per_eng[e]:
                need = {}
                for d in rec["deps"]:
                    p = ins[d]
                    if not need_sync(p, rec):
                        continue
                    key, val = p["tok"]
                    if known.get(key, 0) >= val:
                        continue
                    if need.get(key, 0) < val:
                        need[key] = val
                if rec["dma"]:
                    key, prev = rec["prev_dma"]
                    if prev > 0 and known.get(key, 0) < prev and need.get(key, 0) < prev:
                        need[key] = prev
                for key, val in need.items():
                    eo.wait_ge(sems[key], val)
                    known[key] = val
                    self.nwaits += 1
                bi = rec["fn"](eo)
                self.nins += 1
                if rec["dma"]:
                    bi.then_inc(sems[rec["tok"][0]], 16)
                elif rec["sig"]:
                    bi.then_inc(sems[e], 1)
        finals = dict(self.dcount)
        for e in ENGS:
            if self.cnt[e] > 0:
                finals[e] = self.cnt[e]
        for e in ENGS:
            eo = engobj[e]
            known = self.known[e]
            for key, val in finals.items():
                if known.get(key, 0) < val:
                    eo.wait_ge(sems[key], val)
                    known[key] = val
                    self.nwaits += 1
        self.ins = []
        for b in _ALLBUFS:
            b.w = None
            b.r = []
        del _ALLBUFS[:]
        for t_, b in self.persist:
            _ALLBUFS.append(b)


class Rot:
    def __init__(self, K, name, shape, dt, n, es):
        self.items = []
        for i in range(n):
            t = es.enter_context(K.nc.sbuf_tensor(f"{name}_{K.uid()}", shape, dt))
            self.items.append((t, Buf(f"{name}{i}")))
        self.i = 0

    def next(self):
        it = self.items[self.i % len(self.items)]
        self.i += 1
        return it


class Ctx:
    def __init__(self, nc, cfg):
        self.nc = nc
        self.cfg = cfg
        self._uid = 0
        self.psi = 0

    def uid(self):
        self._uid += 1
        return self._uid

    def sb(self, es, name, shape, dt):
        return es.enter_context(self.nc.sbuf_tensor(f"{name}_{self.uid()}", shape, dt))

    def next_ps(self):
        it = self.ps[self.psi % 8]
        self.psi += 1
        return it


def build(cfg):
    D, DFF, NH, NKV, DEPTH = cfg["D"], cfg["DFF"], cfg["NH"], cfg["NKV"], cfg["DEPTH"]
    DC, FC, NCC = D // 128, DFF // 128, 3 * D // 128
    GQ = NH // NKV
    NHY, NAT = (DEPTH + 1) // 2, DEPTH // 2
    FG = cfg.get("FG", 4)
    assert GQ == 4

    nc = bass.Bass("TRN2", target_bir_lowering=False)
    K = Ctx(nc, cfg)

    def din(name, shape, dt=F32):
        return nc.dram_tensor(name, list(shape), dt, kind="ExternalInput").ap()

    def dscr(name, shape, dt):
        kind = "ExternalOutput" if (cfg.get("debug") and name in ("uT", "cT", "zT", "kfT", "rn_s", "xs")) else "Internal"
        return nc.dram_tensor(name, list(shape), dt, kind=kind).ap()

    x_in = din("xT", [D, T])
    y_out = nc.dram_tensor("yT", [D, T], F32, kind="ExternalOutput").ap()
    gains = din("gains", [128, 2 * DEPTH * DC])
    w_in = din("w_in", [NHY, NCC, 128, DC, 128])
    w_out = din("w_out", [NHY, DC, 128, DC, 128])
    if NAT:
        w_q = din("w_q", [NAT, NH, 128, DC, 128])
        w_k = din("w_k", [NAT, NKV, 128, DC, 128])
        w_v = din("w_v", [NAT, 128, DC, NKV * 128])
        w_o = din("w_o", [NAT, DC, 128, NH, 128])
    w_gu = din("w_gu", [DEPTH, FC, 128, DC, 256])
    w_dn = din("w_dn", [DEPTH, DC, 128, FC, 128])
    hyv = din("hyv", [NHY, 128, 5 * NCC + DC])
    skipr = din("skipr", [NHY, 64, 2 * D])
    fmlp = din("fmlp", [NHY, 64, 64 + 64 + 64 + 4])
    f_wout = din("f_wout", [NHY, 64, 4 * D])
    if NAT:
        atv = din("atv", [NAT, 128, 2 + NH])
        relb = din("relb", [32, NH])
    featsT = din("featsT", [33, NSLOT])
    selfb = din("selfb", [3, NSLOT])
    negdel = din("negdel", [128, DC])
    c_f1s = din("c_f1s", [64, 256])
    c_f1k = din("c_f1k", [128, 256])
    c_tw = din("c_tw", [128, 2, 2, 128])
    c_g = din("c_g", [128, 3, 2, 256])
    c_h = din("c_h", [128, 2, 2, 512])
    c_twc = din("c_twc", [128, 2, 256])
    c_e = din("c_e", [128, 2, 64])
    if NAT:
        c_oh = din("c_oh", [32, 3, 128, 128])
        c_mask = din("c_mask", [128, 3, 128])
    c_seam = din("c_seam", [128, 2])

    xs = dscr("xs", [D, T], F32)
    uT = dscr("uT", [3 * D, T], BF16)
    cT = dscr("cT", [3 * D, T], BF16)
    zT = dscr("zT", [D, T], BF16)
    kfT = dscr("kfT", [2 * D, NSLOT], BF16)
    rn_s = dscr("rn_s", [2 * D], F32)
    kT_s = dscr("kT_s", [NKV * 128, T], BF16)
    vt_s = dscr("vt_s", [T, NKV * 128], BF16)
    b_in_ = dscr("b_w_in", [NHY, NCC, 128, DC, 128], BF16)
    b_out_ = dscr("b_w_out", [NHY, DC, 128, DC, 128], BF16)
    if NAT:
        b_q = dscr("b_w_q", [NAT, NH, 128, DC, 128], BF16)
        b_k = dscr("b_w_k", [NAT, NKV, 128, DC, 128], BF16)
        b_v = dscr("b_w_v", [NAT, 128, DC, NKV * 128], BF16)
        b_o = dscr("b_w_o", [NAT, DC, 128, NH, 128], BF16)
    b_gu = dscr("b_w_gu", [DEPTH, FC, 128, DC, 256], BF16)
    b_dn = dscr("b_w_dn", [DEPTH, DC, 128, FC, 128], BF16)

    with ExitStack() as top:
        sems = {}
        for e in ENGS[:4]:
            sems[e] = top.enter_context(nc.semaphore("s_" + e))
        for e in ENGS:
            for k in range(NDMASEM):
                sems[("dma", e, k)] = top.enter_context(nc.semaphore(f"d_{e}_{k}"))
        fin = top.enter_context(nc.semaphore("fin"))
        P = Prog(nc, sems)
        P.only = cfg.get("only")
        STQ = {0: "pool", 1: "act", 2: "sp"}[cfg.get("stq", 0)]
        P.self_sync = {0: ("act", "dve", "pool"), 1: ("pool",), 2: ("pool", "dve"), 3: ("pool", "act")}[cfg.get("ss", 0)]
        K.ps = []
        for i in range(8):
            t = top.enter_context(nc.psum_tensor(f"psb{i}", [128, 512], F32))
            K.ps.append((t, Buf(f"ps{i}")))
        P.persist = list(K.ps)

        ones_bf = K.sb(top, "ones", [128, 128], BF16)
        gains_t = K.sb(top, "gains", [128, 2 * DEPTH * DC], F32)
        seam_t = K.sb(top, "seam", [128, 2], F32)
        cB = Buf("consts")
        P.op("dve", lambda e: e.memset(ones_bf[:], 1.0), writes=[cB])
        P.dma("sp", gains_t[:], gains[:, :], writes=[cB])
        P.dma("sp", seam_t[:], c_seam[:, :], writes=[cB])

        P.phase("PRO")
        def castw(src, dst, lead):
            idxs = [()]
            for n in lead:
                idxs = [i + (j,) for i in idxs for j in range(n)]
            for ix in idxs:
                s, d_ = src, dst
                for j in ix:
                    s, d_ = s[j], d_[j]
                P.dma("pool", d_, s)

        castw(w_in, b_in_, [NHY, NCC])
        castw(w_out, b_out_, [NHY, DC])
        if NAT:
            castw(w_q, b_q, [NAT, NH])
            castw(w_k, b_k, [NAT, NKV])
            castw(w_v, b_v, [NAT])
            castw(w_o, b_o, [NAT, DC])
        castw(w_gu, b_gu, [DEPTH, FC])
        castw(w_dn, b_dn, [DEPTH, DC])
        P.flush()

        def xview(x):
            return x.rearrange("(c p) t -> p c t", p=128)

        def load_norm(es_bufs, xsrc, t0, gcol):
            xt, xB = es_bufs["x"].next()
            ht, hB = es_bufs["h"].next()
            rs, rsB = es_bufs["rs"].next()
            P.dma("sp", xt[:], xview(xsrc)[:, :, t0:t0 + TT], writes=[xB])
            rms_to(xt, xB, es_bufs["sq"], ht, hB, rs, rsB, gcol)
            return xt, xB, ht, hB

        def rms_to(xt, xB, sqr, ht, hB, rs, rsB, gcol):
            ps, pB = K.next_ps()
            for c in range(DC):
                sq, sqB = sqr.next()
                P.op("act", lambda e, sq=sq, c=c: e.activation(out=sq[:], in_=xt[:, c, :], func=AF.Square),
                     reads=[xB], writes=[sqB])
                P.op("pe", lambda e, c=c, sq=sq: e.matmul(ps[:], ones_bf[:], sq[:], start=(c == 0), stop=(c == DC - 1)),
                     reads=[sqB], writes=[pB])
            P.op("dve", lambda e: e.tensor_scalar(out=rs[:], in0=ps[:], scalar1=1.0 / D, scalar2=EPS,
                                                  op0=ALU.mult, op1=ALU.add), reads=[pB], writes=[rsB])
            P.op("act", lambda e: e.activation(out=rs[:], in_=rs[:], func=AF.Sqrt), reads=[rsB], writes=[rsB])
            P.op("dve", lambda e: e.reciprocal(out=rs[:], in_=rs[:]), reads=[rsB], writes=[rsB])
            for c in range(DC):
                P.op("dve", lambda e, c=c: e.scalar_tensor_tensor(
                    out=ht[:, c, :], in0=xt[:, c, :], scalar=gains_t[:, gcol + c:gcol + c + 1], in1=rs[:],
                    op0=ALU.mult, op1=ALU.mult), reads=[xB, rsB], writes=[hB])

        def ffn_tile(bufs, li, xt, xB):
            ht, hB = bufs["h"].next()
            rs, rsB = bufs["rs"].next()
            rms_to(xt, xB, bufs["sq"], ht, hB, rs, rsB, (DEPTH + li) * DC)
            act, aB = bufs["act"].next()
            for fo in range(FC):
                wt, wB = bufs["wgu"].next()
                P.dma("sp", wt[:], b_gu[li, fo], writes=[wB])
                pg, pgB = K.next_ps()
                pu, puB = K.next_ps()
                for c in range(DC):
                    P.op("pe", lambda e, c=c, wt=wt, pg=pg: e.matmul(pg[:], wt[:, c, 0:128], ht[:, c, :],
                                                                     start=(c == 0), stop=(c == DC - 1)),
                         reads=[wB, hB], writes=[pgB])
                for c in range(DC):
                    P.op("pe", lambda e, c=c, wt=wt, pu=pu: e.matmul(pu[:], wt[:, c, 128:256], ht[:, c, :],
                                                                     start=(c == 0), stop=(c == DC - 1)),
                         reads=[wB, hB], writes=[puB])
                sg, sgB = bufs["sg"].next()
                P.op("act", lambda e, pg=pg, sg=sg: e.activation(out=sg[:], in_=pg[:], func=AF.Silu),
                     reads=[pgB], writes=[sgB])
                P.op("dve", lambda e, pu=pu, sg=sg, fo=fo: e.tensor_tensor(out=act[:, fo, :], in0=pu[:], in1=sg[:],
                                                                           op=ALU.mult),
                     reads=[puB, sgB], writes=[aB])
            for do in range(DC):
                wt, wB = bufs["wdn"].next()
                P.dma("sp", wt[:], b_dn[li, do], writes=[wB])
                py, pyB = K.next_ps()
                for f in range(FC):
                    P.op("pe", lambda e, f=f, wt=wt, py=py: e.matmul(py[:], wt[:, f, :], act[:, f, :],
                                                                     start=(f == 0), stop=(f == FC - 1)),
                         reads=[wB, aB], writes=[pyB])
                P.op("dve", lambda e, py=py, do=do: e.tensor_tensor(out=xt[:, do, :], in0=py[:], in1=xt[:, do, :],
                                                                    op=ALU.add),
                     reads=[pyB, xB], writes=[xB])

        def ffn_bufs(es):
            return dict(
                x=Rot(K, "x", [128, DC, TT], F32, 2, es),
                sq=Rot(K, "sq", [128, TT], BF16, 3, es),
                h=Rot(K, "h", [128, DC, TT], BF16, 1, es),
                rs=Rot(K, "rs", [128, TT], F32, 2, es),
                act=Rot(K, "act", [128, FC, TT], BF16, 1, es),
                wgu=Rot(K, "wgu", [128, DC, 256], BF16, 3, es),
                wdn=Rot(K, "wdn", [128, FC, 128], BF16, 2, es),
                sg=Rot(K, "sg", [128, TT], F32, 2, es),
            )

        def ffn_phase(li, xsrc, xdst):
            P.phase("FFN")
            with ExitStack() as es:
                bufs = ffn_bufs(es)
                for tt in range(NTT):
                    t0 = tt * TT
                    xt, xB = bufs["x"].next()
                    P.dma("sp", xt[:], xview(xsrc)[:, :, t0:t0 + TT], writes=[xB])
                    ffn_tile(bufs, li, xt, xB)
                    P.dma(STQ, xview(xdst)[:, :, t0:t0 + TT], xt[:], reads=[xB])
                P.flush()

        def hyena_layer(li, j, xsrc, xdst):
            P.phase("H1")
            with ExitStack() as es:
                bufs = dict(
                    x=Rot(K, "x", [128, DC, TT], F32, 2, es),
                    sq=Rot(K, "sq", [128, TT], BF16, 3, es),
                    h=Rot(K, "h", [128, DC, TT], BF16, 2, es),
                    rs=Rot(K, "rs", [128, TT], F32, 2, es),
                )
                wr = Rot(K, "w", [128, DC, 128], BF16, 3, es)
                st = Rot(K, "st", [128, TT], BF16, 3, es)
                hv = K.sb(es, "hyv", [128, 5 * NCC + DC], F32)
                hvB = Buf("hv")
                P.dma("sp", hv[:], hyv[j], writes=[hvB])
                for tt in range(NTT):
                    t0 = tt * TT
                    xt, xB, ht, hB = load_norm(bufs, xsrc, t0, li * DC)
                    for oc in range(NCC):
                        wt, wB = wr.next()
                        P.dma("sp", wt[:], b_in_[j, oc], writes=[wB])
                        ps, pB = K.next_ps()
                        for c in range(DC):
                            P.op("pe", lambda e, c=c, wt=wt, ps=ps, ht=ht: e.matmul(
                                ps[:], wt[:, c, :], ht[:, c, :], start=(c == 0), stop=(c == DC - 1)),
                                reads=[wB, hB], writes=[pB])
                        s_, sB = st.next()
                        P.op("act", lambda e, ps=ps, s_=s_, oc=oc: e.activation(
                            out=s_[:], in_=ps[:], func=AF.Identity, bias=hv[:, oc:oc + 1], scale=1.0),
                            reads=[pB, hvB], writes=[sB])
                        P.dma(STQ, uT[oc * 128:(oc + 1) * 128, t0:t0 + TT], s_[:], reads=[sB])
                P.flush()

            P.phase("H2")
            with ExitStack() as es:
                ur = Rot(K, "u", [128, T], BF16, 2, es)
                orr = Rot(K, "o", [128, T], BF16, 2, es)
                hv = K.sb(es, "hyv", [128, 5 * NCC + DC], F32)
                wm = K.sb(es, "wm", [128, 2 * NCC], F32)
                hvB = Buf("hv")
                P.dma("sp", hv[:], hyv[j], writes=[hvB])
                P.op("dve", lambda e: e.tensor_scalar(out=wm[:, 0:NCC], in0=hv[:, NCC:2 * NCC], scalar1=seam_t[:, 0:1],
                                                      scalar2=None, op0=ALU.mult), reads=[hvB], writes=[hvB])
                P.op("dve", lambda e: e.tensor_scalar(out=wm[:, NCC:2 * NCC], in0=hv[:, 3 * NCC:4 * NCC],
                                                      scalar1=seam_t[:, 0:1], scalar2=None, op0=ALU.mult),
                     reads=[hvB], writes=[hvB])
                HS = T // 2
                for cc in range(NCC):
                    u, uB = ur.next()
                    o, oB = orr.next()
                    P.dma("sp", u[:], uT[cc * 128:(cc + 1) * 128, :], writes=[uB])
                    w0 = hv[:, NCC + cc:NCC + cc + 1]
                    w1 = hv[:, 2 * NCC + cc:2 * NCC + cc + 1]
                    w2 = hv[:, 3 * NCC + cc:3 * NCC + cc + 1]
                    cb = hv[:, 4 * NCC + cc:4 * NCC + cc + 1]
                    eng = "dve"
                    P.op(eng, lambda e, u=u, o=o, w1=w1, cb=cb: e.tensor_scalar(
                        out=o[:], in0=u[:], scalar1=w1, scalar2=cb, op0=ALU.mult, op1=ALU.add),
                        reads=[uB, hvB], writes=[oB])
                    P.op(eng, lambda e, u=u, o=o, w0=w0: e.scalar_tensor_tensor(
                        out=o[:, 1:T], in0=u[:, 0:T - 1], scalar=w0, in1=o[:, 1:T], op0=ALU.mult, op1=ALU.add),
                        reads=[uB, hvB], writes=[oB])
                    P.op(eng, lambda e, u=u, o=o, w2=w2: e.scalar_tensor_tensor(
                        out=o[:, 0:T - 1], in0=u[:, 1:T], scalar=w2, in1=o[:, 0:T - 1], op0=ALU.mult, op1=ALU.add),
                        reads=[uB, hvB], writes=[oB])
                    P.op(eng, lambda e, u=u, o=o, cc=cc: e.scalar_tensor_tensor(
                        out=o[:, HS:HS + 1], in0=u[:, HS - 1:HS], scalar=wm[:, cc:cc + 1], in1=o[:, HS:HS + 1],
                        op0=ALU.mult, op1=ALU.add), reads=[uB, hvB], writes=[oB])
                    P.op(eng, lambda e, u=u, o=o, cc=cc: e.scalar_tensor_tensor(
                        out=o[:, HS - 1:HS], in0=u[:, HS:HS + 1], scalar=wm[:, NCC + cc:NCC + cc + 1],
                        in1=o[:, HS - 1:HS], op0=ALU.mult, op1=ALU.add), reads=[uB, hvB], writes=[oB])
                    P.dma(STQ, cT[cc * 128:(cc + 1) * 128, :], o[:], reads=[oB])
                P.flush()

            P.phase("H0")
            with ExitStack() as es:
                mlp = K.sb(es, "mlp", [64, 196], F32)
                wo = K.sb(es, "wo", [64, 4 * D], BF16)
                nd = K.sb(es, "nd", [128, DC], F32)
                sc = K.sb(es, "sc", [64, 4], F32)
                acc = K.sb(es, "acc", [128, 2 * DC * NSEG], F32)
                mB = Buf("mlp")
                accB = Buf("acc")
                P.dma("sp", mlp[:], fmlp[j], writes=[mB])
                P.dma("pool", wo[:], f_wout[j], writes=[mB])
                P.dma("sp", nd[:], negdel[:, :], writes=[mB])
                P.op("dve", lambda e: e.tensor_scalar(out=sc[:, 0:3], in0=mlp[:, 192:195], scalar1=mlp[:, 195:196],
                                                      scalar2=None, op0=ALU.mult), reads=[mB], writes=[mB])
                P.op("dve", lambda e: e.tensor_scalar(out=sc[:, 0:3], in0=sc[:, 0:3], scalar1=16.0 * math.pi, scalar2=None,
                                                      op0=ALU.add), reads=[mB], writes=[mB])
                P.op("pool", lambda e: e.memset(acc[:], 0.0), writes=[accB])
                fr = Rot(K, "ft", [33, SEG], F32, 2, es)
                mr = Rot(K, "mk", [128, 3, SEG], F32, 2, es)
                ar = Rot(K, "a", [64, SEG], F32, 4, es)
                qir = Rot(K, "qi", [64, SEG], mybir.dt.int32, 2, es)
                qfr_ = Rot(K, "qf", [64, SEG], F32, 2, es)
                afr = Rot(K, "af", [64, 2, SEG], BF16, 2, es)
                dr = Rot(K, "dec", [128, SEG], F32, 2, es)
                kr = Rot(K, "ko", [128, SEG], BF16, 4, es)
                jr = Rot(K, "junk", [128, SEG], BF16, 2, es)
                OFF = math.pi + 16.0 * math.pi
                for s in range(NSEG):
                    s0 = s * SEG
                    ft, fB = fr.next()
                    mk, kB = mr.next()
                    P.dma("sp", ft[:], featsT[:, s0:s0 + SEG], writes=[fB])
                    for q in range(3):
                        P.dma("sp", mk[:, q, :], selfb[q, s0:s0 + SEG].partition_broadcast(128), writes=[kB])
                    cur, curB, kk = ft, fB, 33
                    for l in range(3):
                        ps, pB = K.next_ps()
                        if l == 0:
                            lhs = mlp[0:33, 128:192]
                        else:
                            lhs = mlp[:, (l - 1) * 64:l * 64]
                        P.op("pe", lambda e, ps=ps, lhs=lhs, cur=cur, kk=kk: e.matmul(
                            ps[0:64, :], lhs, cur[0:kk, :], start=True, stop=True), reads=[mB, curB], writes=[pB])
                        a, aB = ar.next()
                        qi, qiB = qir.next()
                        qf, qfB = qfr_.next()
                        P.op("dve", lambda e, a=a, ps=ps, l=l: e.tensor_scalar(
                            out=a[:], in0=ps[0:64, :], scalar1=mlp[:, 195:196], scalar2=sc[:, l:l + 1],
                            op0=ALU.mult, op1=ALU.add), reads=[pB, mB], writes=[aB])
                        P.op("dve", lambda e, a=a, qi=qi: e.tensor_scalar(
                            out=qi[:], in0=a[:], scalar1=1.0 / (2.0 * math.pi), scalar2=None, op0=ALU.mult),
                            reads=[aB], writes=[qiB])
                        P.op("dve", lambda e, qi=qi, qf=qf: e.tensor_copy(out=qf[:], in_=qi[:]), reads=[qiB], writes=[qfB])
                        P.op("dve", lambda e, a=a, qf=qf: e.scalar_tensor_tensor(
                            out=a[:], in0=qf[:], scalar=-2.0 * math.pi, in1=a[:], op0=ALU.mult, op1=ALU.add),
                            reads=[aB, qfB], writes=[aB])
                        P.op("dve", lambda e, a=a, qf=qf: e.tensor_scalar(
                            out=qf[:], in0=a[:], scalar1=math.pi, scalar2=2.0 * math.pi, op0=ALU.is_gt, op1=ALU.mult),
                            reads=[aB], writes=[qfB])
                        P.op("dve", lambda e, a=a, qf=qf: e.tensor_tensor(out=a[:], in0=a[:], in1=qf[:], op=ALU.subtract),
                             reads=[aB, qfB], writes=[aB])
                        P.op("act", lambda e, a=a: e.activation(out=a[:], in_=a[:], func=AF.Sin,
                                                                scale=1.0 - 1e-6), reads=[aB, mB], writes=[aB])
                        cur, curB, kk = a, aB, 64
                    af, afB = afr.next()
                    P.op("dve", lambda e, af=af, cur=cur, mk=mk: e.tensor_tensor(
                        out=af[:, 0, :], in0=cur[:], in1=mk[0:64, 0, :], op=ALU.mult), reads=[curB, kB], writes=[afB])
                    P.op("pool", lambda e, af=af, cur=cur, mk=mk: e.tensor_tensor(
                        out=af[:, 1, :], in0=cur[:], in1=mk[0:64, 1, :], op=ALU.mult), reads=[curB, kB], writes=[afB])
                    for cc in range(DC):
                        dec, dB = dr.next()
                        P.op("act", lambda e, dec=dec, mk=mk, cc=cc: e.activation(
                            out=dec[:], in_=mk[:, 2, :], func=AF.Exp, scale=nd[:, cc:cc + 1]),
                            reads=[kB, mB], writes=[dB])
                        for o_ in range(2):
                            ps, pB = K.next_ps()
                            cF = (o_ * 2 + 0) * D + cc * 128
                            cBk = (o_ * 2 + 1) * D + cc * 128
                            P.op("pe", lambda e, ps=ps, af=af, cF=cF: e.matmul(
                                ps[:], wo[:, cF:cF + 128], af[:, 0, :], start=True, stop=False),
                                reads=[mB, afB], writes=[pB])
                            P.op("pe", lambda e, ps=ps, af=af, cBk=cBk: e.matmul(
                                ps[:], wo[:, cBk:cBk + 128], af[:, 1, :], start=False, stop=True),
                                reads=[mB, afB], writes=[pB])
                            ko, koB = kr.next()
                            P.op("dve", lambda e, ko=ko, ps=ps, dec=dec: e.tensor_tensor(
                                out=ko[:], in0=ps[:], in1=dec[:], op=ALU.mult), reads=[pB, dB], writes=[koB])
                            jk, jB = jr.next()
                            col = (o_ * DC + cc) * NSEG + s
                            P.op("act", lambda e, ko=ko, jk=jk, col=col: e.activation(
                                out=jk[:], in_=ko[:], func=AF.Abs, accum_out=acc[:, col:col + 1]),
                                reads=[koB, accB], writes=[jB, accB])
                            r0 = o_ * D + cc * 128
                            P.dma(STQ, kfT[r0:r0 + 128, s0:s0 + SEG], ko[:], reads=[koB])
                rn = K.sb(es, "rn", [128, 2 * DC], F32)
                P.op("dve", lambda e: e.tensor_reduce(out=rn[:], in_=acc[:].rearrange("p (a s) -> p a s", s=NSEG),
                                                      axis=AX.X, op=ALU.add), reads=[accB], writes=[accB])
                P.op("dve", lambda e: e.reciprocal(out=rn[:], in_=rn[:]), reads=[accB], writes=[accB])
                with nc.allow_non_contiguous_dma("tiny"):
                    P.dma("sp", rn_s.rearrange("(a p) -> p a", p=128), rn[:], reads=[accB])
                    P.flush()

            P.phase("H3")
            with ExitStack() as es:
                f1s = K.sb(es, "f1s", [64, 256], BF16)
                f1k = K.sb(es, "f1k", [128, 256], BF16)
                tw = K.sb(es, "tw", [128, 2, 2, 128], F32)
                gm = K.sb(es, "gm", [128, 3, 2, 256], BF16)
                hm = K.sb(es, "hm", [128, 2, 2, 512], BF16)
                twc = K.sb(es, "twc", [128, 2, 256], F32)
                em = K.sb(es, "em", [128, 2, 64], BF16)
                rnr = Rot(K, "rnb", [128, 2, FG], F32, 2, es)
                skr = Rot(K, "skp", [64, 2, FG], F32, 2, es)
                kB_ = Buf("fftc")
                for dst, src in ((f1s, c_f1s), (f1k, c_f1k), (gm, c_g), (hm, c_h), (em, c_e)):
                    P.dma("pool", dst[:], src, writes=[kB_])
                P.dma("sp", tw[:], c_tw, writes=[kB_])
                P.dma("sp", twc[:], c_twc, writes=[kB_])

                dp = H3_DEPTH if cfg.get("ilv", 0) else dict(sig=2, ker=1, asb=3, tmp=6, y=2, kf=1, p=2, z=1, z1=1, gt=4)
                sigr = Rot(K, "sig", [64, 3, FG, 256], BF16, dp["sig"], es)
                kerr = Rot(K, "ker", [128, 2, FG, 256], BF16, dp["ker"], es)
                asb = Rot(K, "asb", [128, 4, 512], F32, dp["asb"], es)
                tmp = Rot(K, "tmp", [128, 4, 256], F32, dp["tmp"], es)
                yr = Rot(K, "y", [128, 2, 2, FG, 128], BF16, dp["y"], es)
                kfr = Rot(K, "kf", [128, 2, 2, 2, FG, 128], BF16, dp["kf"], es)
                pr = Rot(K, "p", [128, 2, 2, FG, 128], BF16, dp["p"], es)
                zr = Rot(K, "z", [128, 2, FG, 256], BF16, dp["z"], es)
                z1r = Rot(K, "z1", [64, FG, 256], BF16, dp["z1"], es)
                z2r = Rot(K, "z2", [64, FG, 256], BF16, dp["z1"], es)
                gt = Rot(K, "gt", [64, 2, 256], F32, dp["gt"], es)

                def inter(*gens):
                    gens = [g_ for g_ in gens if g_ is not None]
                    while gens:
                        for g_ in list(gens):
                            try:
                                next(g_)
                            except StopIteration:
                                gens.remove(g_)
                            yield

                def cmul6(a_re, a_im, b_re, b_im, o_re, o_im, v, rds, oB):
                    t1, t1B = tmp.next()
                    t2, t2B = tmp.next()
                    t3, t3B = tmp.next()
                    t4, t4B = tmp.next()
                    P.op("dve", lambda e: e.tensor_tensor(out=v(t1), in0=a_re, in1=b_re, op=ALU.mult), reads=rds, writes=[t1B])
                    P.op("dve", lambda e: e.tensor_tensor(out=v(t2), in0=a_im, in1=b_im, op=ALU.mult), reads=rds, writes=[t2B])
                    P.op("dve" if cfg.get("bal", 0) else "pool",
                         lambda e: e.tensor_tensor(out=v(t3), in0=a_re, in1=b_im, op=ALU.mult), reads=rds, writes=[t3B])
                    P.op("dve" if cfg.get("bal", 0) >= 2 else "pool",
                         lambda e: e.tensor_tensor(out=v(t4), in0=a_im, in1=b_re, op=ALU.mult), reads=rds, writes=[t4B])
                    P.op("dve", lambda e: e.tensor_tensor(out=o_re, in0=v(t1), in1=v(t2), op=ALU.subtract),
                         reads=[t1B, t2B], writes=[oB])
                    P.op("dve" if cfg.get("bal", 0) >= 3 else "pool",
                         lambda e: e.tensor_tensor(out=o_im, in0=v(t3), in1=v(t4), op=ALU.add),
                         reads=[t3B, t4B], writes=[oB])

                def fwd_fft_g(src, sB, kk, f1, y, yB):
                    for c0 in range(0, FG, 4):
                        a, aB = asb.next()
                        for ci in range(4):
                            c = c0 + ci
                            ps, pB = K.next_ps()
                            for hf in range(2):
                                lh = src(c)[:, hf * 128:(hf + 1) * 128]
                                P.op("pe", lambda e, ps=ps, lh=lh, hf=hf: e.matmul(
                                    ps[:, hf * 256:(hf + 1) * 256], lh, f1[0:kk, :],
                                    start=True, stop=True), reads=[sB, kB_], writes=[pB])
                            P.op("act", lambda e, ps=ps, a=a, ci=ci: e.copy(out=a[:, ci, :], in_=ps[:]),
                                 reads=[pB], writes=[aB])
                        yield
                        av = a[:].rearrange("p c (h q r) -> p c h q r", h=2, q=2)
                        Ar, Ai = av[:, :, :, 0, :], av[:, :, :, 1, :]
                        Tr = tw[:, :, 0, :].unsqueeze(1).to_broadcast([128, 4, 2, 128])
                        Ti = tw[:, :, 1, :].unsqueeze(1).to_broadcast([128, 4, 2, 128])
                        yre = y[:, 0, :, c0:c0 + 4, :].rearrange("p h c r -> p c h r")
                        yim = y[:, 1, :, c0:c0 + 4, :].rearrange("p h c r -> p c h r")
                        cmul6(Ar, Ai, Tr, Ti, yre, yim, lambda t: t[:].rearrange("p c (h r) -> p c h r", h=2),
                              [aB, kB_], yB)
                        yield

                def stage3(y, yB, c0):
                    outs = []
                    for part in range(2):
                        for kh in range(2):
                            ps, pB = K.next_ps()
                            n = 0
                            for hf in range(2):
                                for q in range(2):
                                    if part == 0:
                                        gsel = 0 if q == 0 else 2
                                    else:
                                        gsel = 1 if q == 0 else 0
                                    P.op("pe", lambda e, ps=ps, gsel=gsel, hf=hf, kh=kh, q=q, n=n: e.matmul(
                                        ps[:], gm[:, gsel, hf, kh * 128:(kh + 1) * 128],
                                        y[:, q, hf, c0:c0 + 4, :], start=(n == 0), stop=(n == 3)),
                                        reads=[yB, kB_], writes=[pB])
                                    n += 1
                            outs.append((ps, pB))
                    return outs

                def kernel_spec_g(o_, ker, krB, kf, kfB, ch0, rnb, rnB):
                    y, yB = yr.next()
                    yield from fwd_fft_g(lambda c: ker[:, o_, c, :], krB, 128, f1k, y, yB)
                    for c0 in range(0, FG, 4):
                        outs = stage3(y, yB, c0)
                        i = 0
                        for part in range(2):
                            for kh in range(2):
                                ps, pB = outs[i]
                                i += 1
                                rb = rnb[:, o_, c0:c0 + 4].unsqueeze(2).to_broadcast([128, 4, 128])
                                P.op("dve", lambda e, ps=ps, part=part, kh=kh, rb=rb, c0=c0: e.tensor_tensor(
                                    out=kf[:, o_, part, kh, c0:c0 + 4, :],
                                    in0=ps[:].rearrange("p (c r) -> p c r", c=4), in1=rb, op=ALU.mult),
                                    reads=[pB, rnB], writes=[kfB])
                        yield

                def sig_fwd_mult_g(zin, zinB, o_, kf, kfB, pt, ptB):
                    y, yB = yr.next()
                    yield from fwd_fft_g(zin, zinB, 64, f1s, y, yB)
                    for c0 in range(0, FG, 4):
                        outs = stage3(y, yB, c0)
                        for kh in range(2):
                            (pxr, pxrB), (pxi, pxiB) = outs[kh], outs[2 + kh]
                            a, aB = asb.next()
                            P.op("act", lambda e, a=a, pxr=pxr: e.copy(out=a[:, 0, :], in_=pxr[:]),
                                 reads=[pxrB], writes=[aB])
                            P.op("act", lambda e, a=a, pxi=pxi: e.copy(out=a[:, 1, :], in_=pxi[:]),
                                 reads=[pxiB], writes=[aB])
                            Xr = a[:, 0, :].rearrange("p (c r) -> p c r", c=4)
                            Xi = a[:, 1, :].rearrange("p (c r) -> p c r", c=4)
                            Kr = kf[:, o_, 0, kh, c0:c0 + 4, :]
                            Ki = kf[:, o_, 1, kh, c0:c0 + 4, :]
                            cmul6(Xr, Xi, Kr, Ki, pt[:, 0, kh, c0:c0 + 4, :], pt[:, 1, kh, c0:c0 + 4, :],
                                  lambda t: t[:, 0:2, :].rearrange("p a (b r) -> p (a b) r", b=2), [aB, kfB], ptB)
                        yield

                def inverse_gate_g(o_, pt, ptB, zi_fn, zinB, sig, sgB, zo, zoB, ch0, skp, skB):
                    z, zB = zr.next()
                    for c0 in range(0, FG, 2):
                        a, aB = asb.next()
                        for ci in range(2):
                            c = c0 + ci
                            ps, pB = K.next_ps()
                            n = 0
                            for kh in range(2):
                                for q in range(2):
                                    P.op("pe", lambda e, ps=ps, kh=kh, q=q, c=c, n=n: e.matmul(
                                        ps[:], pt[:, q, kh, c, :], hm[:, q, kh, :], start=(n == 0), stop=(n == 3)),
                                        reads=[ptB, kB_], writes=[pB])
                                    n += 1
                            P.op("act", lambda e, ps=ps, a=a, ci=ci: e.copy(out=a[:, ci, :], in_=ps[:]),
                                 reads=[pB], writes=[aB])
                        yield
                        av = a[:, 0:2, :].rearrange("p c (q n) -> p c q n", q=2)
                        Qr, Qi = av[:, :, 0, :], av[:, :, 1, :]
                        Cr = twc[:, 0, :].unsqueeze(1).to_broadcast([128, 2, 256])
                        Ci = twc[:, 1, :].unsqueeze(1).to_broadcast([128, 2, 256])
                        cmul6(Qr, Qi, Cr, Ci, z[:, 0, c0:c0 + 2, :], z[:, 1, c0:c0 + 2, :],
                              lambda t: t[:, 0:2, :], [aB, kB_], zB)
                        yield
                    for c0 in range(0, FG, 2):
                        ps, pB = K.next_ps()
                        P.op("pe", lambda e, ps=ps, c0=c0: e.matmul(
                            ps[0:64, :], em[:, 0, :], z[:, 0, c0:c0 + 2, :], start=True, stop=False),
                            reads=[zB, kB_], writes=[pB])
                        P.op("pe", lambda e, ps=ps, c0=c0: e.matmul(
                            ps[0:64, :], em[:, 1, :], z[:, 1, c0:c0 + 2, :], start=False, stop=True),
                            reads=[zB, kB_], writes=[pB])
                        g1, g1B = gt.next()
                        sk = skp[:, o_, c0:c0 + 2].unsqueeze(2).to_broadcast([64, 2, 256])
                        zi_ap = zi_fn(c0)
                        gate_ap = sig[:, 1 + o_, c0:c0 + 2, :]
                        P.op("dve" if cfg.get("bal", 0) >= 4 else "pool", lambda e, g1=g1, zi_ap=zi_ap, sk=sk: e.tensor_tensor(
                            out=g1[:], in0=zi_ap, in1=sk, op=ALU.mult), reads=[zinB, skB], writes=[g1B])
                        P.op("dve", lambda e, g1=g1, ps=ps: e.tensor_tensor(
                            out=g1[:], in0=ps[0:64, :].rearrange("p (c n) -> p c n", c=2), in1=g1[:], op=ALU.add),
                            reads=[pB, g1B], writes=[g1B])
                        P.op("dve" if cfg.get("bal", 0) >= 4 else "pool", lambda e, g1=g1, gate_ap=gate_ap, c0=c0: e.tensor_tensor(
                            out=zo[:, c0:c0 + 2, :], in0=g1[:], in1=gate_ap, op=ALU.mult),
                            reads=[g1B, sgB], writes=[zoB])
                        yield

                NG = D // FG
                state = {}

                def stageA(g):
                    ch0 = g * FG
                    sig, sgB = sigr.next()
                    ker, krB = kerr.next()
                    for q in range(3):
                        P.dma("sp", sig[:, q, :, :],
                              cT[q * D + ch0:q * D + ch0 + FG, :].rearrange("c (a b) -> a c b", b=256), writes=[sgB])
                    for o_ in range(2):
                        P.dma("sp", ker[:, o_, :, :],
                              kfT[o_ * D + ch0:o_ * D + ch0 + FG, :].rearrange("c (a b) -> a c b", b=256), writes=[krB])
                    rnb, rnB = rnr.next()
                    skp, skB = skr.next()
                    for o_ in range(2):
                        P.dma("sp", rnb[:, o_, :], rn_s[o_ * D + ch0:o_ * D + ch0 + FG].partition_broadcast(128), writes=[rnB])
                        P.dma("sp", skp[:, o_, :], skipr[j, :, o_ * D + ch0:o_ * D + ch0 + FG], writes=[skB])
                    kf, kfB = kfr.next()
                    pt0, pt0B = pr.next()
                    state[g] = (sig, sgB, kf, kfB, pt0, pt0B, skp, skB)
                    yield
                    yield from inter(kernel_spec_g(0, ker, krB, kf, kfB, ch0, rnb, rnB),
                                     kernel_spec_g(1, ker, krB, kf, kfB, ch0, rnb, rnB))
                    yield from sig_fwd_mult_g(lambda c: sig[:, 0, c, :], sgB, 0, kf, kfB, pt0, pt0B)

                def stageB(g):
                    ch0 = g * FG
                    sig, sgB, kf, kfB, pt0, pt0B, skp, skB = state.pop(g)
                    z1, z1B = z1r.next()
                    yield from inverse_gate_g(0, pt0, pt0B, lambda c0: sig[:, 0, c0:c0 + 2, :], sgB, sig, sgB, z1, z1B, ch0, skp, skB)
                    pt1, pt1B = pr.next()
                    yield from sig_fwd_mult_g(lambda c: z1[:, c, :], z1B, 1, kf, kfB, pt1, pt1B)
                    z2, z2B = z2r.next()
                    yield from inverse_gate_g(1, pt1, pt1B, lambda c0: z1[:, c0:c0 + 2, :], z1B, sig, sgB, z2, z2B, ch0, skp, skB)
                    P.dma(STQ, zT[ch0:ch0 + FG, :].rearrange("c (a b) -> a c b", b=256), z2[:], reads=[z2B])
                    yield

                ILV = cfg.get("ilv", 0)
                for i in range(NG + 1):
                    ga = stageA(i) if i < NG else None
                    gb = stageB(i - 1) if i > 0 else None
                    if ILV == 2:
                        continue
                    if ILV:
                        for _ in inter(ga, gb):
                            pass
                    else:
                        for g_ in (gb, ga):
                            if g_ is not None:
                                for _ in g_:
                                    pass
                if ILV == 2:
                    def full(g):
                        yield from stageA(g)
                        yield from stageB(g)
                    for g in range(0, NG, 2):
                        for _ in inter(full(g), full(g + 1)):
                            pass
                P.flush()

            P.phase("H4")
            with ExitStack() as es:
                bufs = dict(x=Rot(K, "x", [128, DC, TT], F32, 2, es))
                zr_ = Rot(K, "zt", [128, DC, TT], BF16, 2, es)
                wr = Rot(K, "w", [128, DC, 128], BF16, 3, es)
                hv = K.sb(es, "hyv", [128, 5 * NCC + DC], F32)
                hvB = Buf("hv")
                P.dma("sp", hv[:], hyv[j], writes=[hvB])
                for tt in range(NTT):
                    t0 = tt * TT
                    xt, xB = bufs["x"].next()
                    zt, zB = zr_.next()
                    P.dma("sp", xt[:], xview(xsrc)[:, :, t0:t0 + TT], writes=[xB])
                    P.dma("sp", zt[:], xview(zT)[:, :, t0:t0 + TT], writes=[zB])
                    for oc in range(DC):
                        wt, wB = wr.next()
                        P.dma("sp", wt[:], b_out_[j, oc], writes=[wB])
                        ps, pB = K.next_ps()
                        for c in range(DC):
                            P.op("pe", lambda e, c=c, wt=wt, ps=ps, zt=zt: e.matmul(
                                ps[:], wt[:, c, :], zt[:, c, :], start=(c == 0), stop=(c == DC - 1)),
                                reads=[wB, zB], writes=[pB])
                        P.op("dve", lambda e, ps=ps, xt=xt, oc=oc: e.scalar_tensor_tensor(
                            out=xt[:, oc, :], in0=ps[:], scalar=hv[:, 5 * NCC + oc:5 * NCC + oc + 1], in1=xt[:, oc, :],
                            op0=ALU.add, op1=ALU.add), reads=[pB, xB, hvB], writes=[xB])
                    P.dma(STQ, xview(xs)[:, :, t0:t0 + TT], xt[:], reads=[xB])
                P.flush()
            ffn_phase(li, xs, xdst)

        def attn_layer(li, j, xsrc, xdst):
            NB = T // 128
            P.phase("A1")
            with ExitStack() as es:
                bufs = dict(
                    x=Rot(K, "x", [128, DC, TT], F32, 2, es),
                    sq=Rot(K, "sq", [128, TT], BF16, 3, es),
                    h=Rot(K, "h", [128, DC, TT], BF16, 2, es),
                    rs=Rot(K, "rs", [128, TT], F32, 2, es),
                )
                wr = Rot(K, "w", [128, DC, 128], BF16, 3, es)
                wv = K.sb(es, "wv", [128, DC, NKV * 128], BF16)
                av = K.sb(es, "atv", [128, 2 + NH], F32)
                cB2 = Buf("c")
                P.dma("sp", wv[:], b_v[j], writes=[cB2])
                P.dma("sp", av[:], atv[j], writes=[cB2])
                sqk = Rot(K, "sqk", [128, TT], BF16, 2, es)
                kst = Rot(K, "kst", [128, TT], BF16, 2, es)
                rk = Rot(K, "rk", [128, TT], F32, 2, es)
                ksb = Rot(K, "ksb", [128, TT], F32, 2, es)
                vst = Rot(K, "vst", [128, NKV * 128], BF16, 3, es)
                for tt in range(NTT):
                    t0 = tt * TT
                    xt, xB, ht, hB = load_norm(bufs, xsrc, t0, li * DC)
                    for kh in range(NKV):
                        wt, wB = wr.next()
                        P.dma("sp", wt[:], b_k[j, kh], writes=[wB])
                        ps, pB = K.next_ps()
                        for c in range(DC):
                            P.op("pe", lambda e, c=c, wt=wt, ps=ps, ht=ht: e.matmul(
                                ps[:], wt[:, c, :], ht[:, c, :], start=(c == 0), stop=(c == DC - 1)),
                                reads=[wB, hB], writes=[pB])
                        kf_, kfB_ = ksb.next()
                        sq_, sqB_ = sqk.next()
                        P.op("act", lambda e, ps=ps, kf_=kf_: e.copy(out=kf_[:], in_=ps[:]), reads=[pB], writes=[kfB_])
                        P.op("act", lambda e, kf_=kf_, sq_=sq_: e.activation(out=sq_[:], in_=kf_[:], func=AF.Square),
                             reads=[kfB_], writes=[sqB_])
                        ps2, p2B = K.next_ps()
                        P.op("pe", lambda e, ps2=ps2, sq_=sq_: e.matmul(ps2[:], ones_bf[:], sq_[:], start=True, stop=True),
                             reads=[sqB_], writes=[p2B])
                        r_, rB_ = rk.next()
                        P.op("dve", lambda e, r_=r_, ps2=ps2: e.tensor_scalar(
                            out=r_[:], in0=ps2[:], scalar1=1.0 / HD, scalar2=EPS, op0=ALU.mult, op1=ALU.add),
                            reads=[p2B], writes=[rB_])
                        P.op("act", lambda e, r_=r_: e.activation(out=r_[:], in_=r_[:], func=AF.Sqrt), reads=[rB_], writes=[rB_])
                        P.op("dve", lambda e, r_=r_: e.reciprocal(out=r_[:], in_=r_[:]), reads=[rB_], writes=[rB_])
                        ks_, ksB_ = kst.next()
                        P.op("dve", lambda e, ks_=ks_, kf_=kf_, r_=r_: e.scalar_tensor_tensor(
                            out=ks_[:], in0=kf_[:], scalar=av[:, 1:2], in1=r_[:], op0=ALU.mult, op1=ALU.mult),
                            reads=[kfB_, rB_, cB2], writes=[ksB_])
                        P.dma(STQ, kT_s[kh * 128:(kh + 1) * 128, t0:t0 + TT], ks_[:], reads=[ksB_])
                    for b in range(TT // 128):
                        ps, pB = K.next_ps()
                        for c in range(DC):
                            P.op("pe", lambda e, c=c, ps=ps, ht=ht, b=b: e.matmul(
                                ps[:, 0:NKV * 128], ht[:, c, b * 128:(b + 1) * 128], wv[:, c, :],
                                start=(c == 0), stop=(c == DC - 1)), reads=[cB2, hB], writes=[pB])
                        v_, vB_ = vst.next()
                        P.op("act", lambda e, ps=ps, v_=v_: e.copy(out=v_[:], in_=ps[:, 0:NKV * 128]),
                             reads=[pB], writes=[vB_])
                        P.dma(STQ, vt_s[t0 + b * 128:t0 + (b + 1) * 128, :], v_[:], reads=[vB_])
                P.flush()

            P.phase("A2")
            with ExitStack() as es:
                bufs = dict(
                    x=Rot(K, "x", [128, DC, TT], F32, 1, es),
                    sq=Rot(K, "sq", [128, TT], BF16, 3, es),
                    h=Rot(K, "h", [128, DC, TT], BF16, 1, es),
                    rs=Rot(K, "rs", [128, TT], F32, 2, es),
                )
                wr = Rot(K, "w", [128, DC, 128], BF16, 3, es)
                wor = Rot(K, "wo", [128, NH, 128], BF16, 2, es)
                av = K.sb(es, "atv", [128, 2 + NH], F32)
                esk = K.sb(es, "esk", [128, NH], F32)
                qg = K.sb(es, "qg", [128, 1], F32)
                bias = K.sb(es, "bias", [128, 3, NH, 128], F32)
                msk = K.sb(es, "msk", [128, 3, 128], F32)
                rb_t = K.sb(es, "relb", [32, NH], F32)
                cB2 = Buf("c")
                bB = Buf("bias")
                P.dma("sp", av[:], atv[j], writes=[cB2])
                P.dma("sp", msk[:], c_mask, writes=[cB2])
                P.dma("sp", rb_t[:], relb, writes=[cB2])
                P.op("act", lambda e: e.activation(out=esk[:], in_=av[:, 2:2 + NH], func=AF.Exp), reads=[cB2], writes=[cB2])
                P.op("dve", lambda e: e.tensor_scalar(out=qg[:], in0=av[:, 0:1], scalar1=HD ** -0.5, scalar2=None,
                                                      op0=ALU.mult), reads=[cB2], writes=[cB2])
                ohr = Rot(K, "oh", [32, 32, 128], F32, 1, es)
                for kb in range(3):
                    for q0 in range(0, 128, 32):
                        oh, ohB = ohr.next()
                        P.dma("sp", oh[:], c_oh[:, kb, q0:q0 + 32, :], writes=[ohB])
                        ps, pB = K.next_ps()
                        for qi in range(32):
                            P.op("pe", lambda e, ps=ps, oh=oh, qi=qi: e.matmul(
                                ps[:, qi * NH:(qi + 1) * NH], oh[:, qi, :], rb_t[:, :], start=True, stop=True),
                                reads=[ohB, cB2], writes=[pB])
                        P.op("dve", lambda e, ps=ps, kb=kb, q0=q0: e.tensor_tensor(
                            out=bias[:, kb, :, q0:q0 + 32],
                            in0=ps[:, 0:32 * NH].rearrange("p (q h) -> p h q", h=NH),
                            in1=msk[:, kb, q0:q0 + 32].unsqueeze(1).to_broadcast([128, NH, 32]), op=ALU.add),
                            reads=[pB, cB2], writes=[bB])
                kwr = Rot(K, "kw", [128, TT + 256], BF16, 2, es)
                vwr = Rot(K, "vw", [128, TT // 128 + 2, 128], BF16, 2, es)
                qnr = Rot(K, "qn", [128, GQ, TT], BF16, 2, es)
                qfr = Rot(K, "qf", [128, TT], F32, 2, es)
                sqq = Rot(K, "sqq", [128, TT], BF16, 2, es)
                rq = Rot(K, "rq", [128, TT], F32, 2, es)
                sbr = Rot(K, "sb", [128, 512], F32, 3, es)
                ptr = Rot(K, "pT", [128, 3, 512], BF16, 2, es)
                dnr = Rot(K, "den", [128, 512], F32, 2, es)
                otr = Rot(K, "oT", [128, NH, TT], BF16, 1, es)
                for tt in range(NTT):
                    t0 = tt * TT
                    xt, xB, ht, hB = load_norm(bufs, xsrc, t0, li * DC)
                    oT, oB = otr.next()
                    lo = max(t0 - 128, 0)
                    hi = min(t0 + TT + 128, T)
                    off = lo - (t0 - 128)
                    for kh in range(NKV):
                        kw, kwB = kwr.next()
                        vw, vwB = vwr.next()
                        P.dma("sp", kw[:, off:off + hi - lo], kT_s[kh * 128:(kh + 1) * 128, lo:hi], writes=[kwB])
                        P.dma("sp", vw[:, off // 128:off // 128 + (hi - lo) // 128, :],
                              vt_s[lo:hi, kh * 128:(kh + 1) * 128].rearrange("(b p) d -> p b d", p=128), writes=[vwB])
                        qn, qnB = qnr.next()
                        for gi in range(GQ):
                            h_ = kh * GQ + gi
                            wt, wB = wr.next()
                            P.dma("sp", wt[:], b_q[j, h_], writes=[wB])
                            ps, pB = K.next_ps()
                            for c in range(DC):
                                P.op("pe", lambda e, c=c, wt=wt, ps=ps, ht=ht: e.matmul(
                                    ps[:], wt[:, c, :], ht[:, c, :], start=(c == 0), stop=(c == DC - 1)),
                                    reads=[wB, hB], writes=[pB])
                            qf, qfB = qfr.next()
                            sq_, sqB_ = sqq.next()
                            P.op("act", lambda e, ps=ps, qf=qf: e.copy(out=qf[:], in_=ps[:]), reads=[pB], writes=[qfB])
                            P.op("act", lambda e, qf=qf, sq_=sq_: e.activation(out=sq_[:], in_=qf[:], func=AF.Square),
                                 reads=[qfB], writes=[sqB_])
                            ps2, p2B = K.next_ps()
                            P.op("pe", lambda e, ps2=ps2, sq_=sq_: e.matmul(ps2[:], ones_bf[:], sq_[:], start=True, stop=True),
                                 reads=[sqB_], writes=[p2B])
                            r_, rB_ = rq.next()
                            P.op("dve", lambda e, r_=r_, ps2=ps2: e.tensor_scalar(
                                out=r_[:], in0=ps2[:], scalar1=1.0 / HD, scalar2=EPS, op0=ALU.mult, op1=ALU.add),
                                reads=[p2B], writes=[rB_])
                            P.op("act", lambda e, r_=r_: e.activation(out=r_[:], in_=r_[:], func=AF.Sqrt), reads=[rB_], writes=[rB_])
                            P.op("dve", lambda e, r_=r_: e.reciprocal(out=r_[:], in_=r_[:]), reads=[rB_], writes=[rB_])
                            P.op("dve", lambda e, qn=qn, gi=gi, qf=qf, r_=r_: e.scalar_tensor_tensor(
                                out=qn[:, gi, :], in0=qf[:], scalar=qg[:, 0:1], in1=r_[:], op0=ALU.mult, op1=ALU.mult),
                                reads=[qfB, rB_, cB2], writes=[qnB])
                        for jb in range(TT // 128):
                            n = t0 // 128 + jb
                            kbs = [kb for kb in range(3) if 0 <= n - 1 + kb < NB]
                            pT, pTB = ptr.next()
                            for kb in kbs:
                                ps, pB = K.next_ps()
                                P.op("pe", lambda e, ps=ps, kw=kw, qn=qn, jb=jb, kb=kb: e.matmul(
                                    ps[:].rearrange("p (g q) -> p g q", g=GQ),
                                    kw[:, (jb + kb) * 128:(jb + kb + 1) * 128],
                                    qn[:, :, jb * 128:(jb + 1) * 128], start=True, stop=True),
                                    reads=[kwB, qnB], writes=[pB])
                                s_, sB_ = sbr.next()
                                P.op("dve", lambda e, s_=s_, ps=ps, kb=kb, kh=kh: e.tensor_tensor(
                                    out=s_[:].rearrange("p (g q) -> p g q", g=GQ),
                                    in0=ps[:].rearrange("p (g q) -> p g q", g=GQ),
                                    in1=bias[:, kb, kh * GQ:(kh + 1) * GQ, :], op=ALU.add),
                                    reads=[pB, bB], writes=[sB_])
                                seam = (n == NB // 2 and kb == 0) or (n == NB // 2 - 1 and kb == 2)
                                if seam:
                                    P.op("act", lambda e, s_=s_, pT=pT, kb=kb: e.activation(
                                        out=pT[:, kb, :], in_=s_[:], func=AF.Exp, bias=seam_t[:, 1:2], scale=1.0),
                                        reads=[sB_], writes=[pTB])
                                else:
                                    P.op("act", lambda e, s_=s_, pT=pT, kb=kb: e.activation(
                                        out=pT[:, kb, :], in_=s_[:], func=AF.Exp), reads=[sB_], writes=[pTB])
                            pd, pdB = K.next_ps()
                            po, poB = K.next_ps()
                            for i, kb in enumerate(kbs):
                                P.op("pe", lambda e, pd=pd, pT=pT, kb=kb, i=i: e.matmul(
                                    pd[:], ones_bf[:], pT[:, kb, :], start=(i == 0), stop=(i == len(kbs) - 1)),
                                    reads=[pTB], writes=[pdB])
                            for i, kb in enumerate(kbs):
                                P.op("pe", lambda e, po=po, pT=pT, kb=kb, i=i, vw=vw, jb=jb: e.matmul(
                                    po[:], vw[:, jb + kb, :], pT[:, kb, :], start=(i == 0), stop=(i == len(kbs) - 1)),
                                    reads=[pTB, vwB], writes=[poB])
                            dn, dnB = dnr.next()
                            P.op("dve", lambda e, dn=dn, pd=pd, kh=kh: e.tensor_tensor(
                                out=dn[:].rearrange("p (g q) -> p g q", g=GQ),
                                in0=pd[:].rearrange("p (g q) -> p g q", g=GQ),
                                in1=esk[:, kh * GQ:(kh + 1) * GQ].unsqueeze(2).to_broadcast([128, GQ, 128]), op=ALU.add),
                                reads=[pdB, cB2], writes=[dnB])
                            P.op("dve", lambda e, dn=dn: e.reciprocal(out=dn[:], in_=dn[:]), reads=[dnB], writes=[dnB])
                            P.op("dve", lambda e, dn=dn, po=po, oT=oT, kh=kh, jb=jb: e.tensor_tensor(
                                out=oT[:, kh * GQ:(kh + 1) * GQ, jb * 128:(jb + 1) * 128],
                                in0=po[:].rearrange("p (g q) -> p g q", g=GQ),
                                in1=dn[:].rearrange("p (g q) -> p g q", g=GQ), op=ALU.mult),
                                reads=[poB, dnB], writes=[oB])
                    for oc in range(DC):
                        wt, wB = wor.next()
                        P.dma("sp", wt[:], b_o[j, oc], writes=[wB])
                        ps, pB = K.next_ps()
                        for h_ in range(NH):
                            P.op("pe", lambda e, h_=h_, wt=wt, ps=ps, oT=oT: e.matmul(
                                ps[:], wt[:, h_, :], oT[:, h_, :], start=(h_ == 0), stop=(h_ == NH - 1)),
                                reads=[wB, oB], writes=[pB])
                        P.op("dve", lambda e, ps=ps, xt=xt, oc=oc: e.tensor_tensor(
                            out=xt[:, oc, :], in0=ps[:], in1=xt[:, oc, :], op=ALU.add), reads=[pB, xB], writes=[xB])
                    P.dma(STQ, xview(xs)[:, :, t0:t0 + TT], xt[:], reads=[xB])
                P.flush()
            ffn_phase(li, xs, xdst)

        for li in range(DEPTH):
            src = x_in if li == 0 else xs
            dst = y_out if li == DEPTH - 1 else xs
            if li % 2 == 0:
                hyena_layer(li, li // 2, src, dst)
            else:
                attn_layer(li, li // 2, src, dst)
        nc.tensor.sem_inc(fin, 1)
        nc.scalar.sem_inc(fin, 1)
        nc.vector.sem_inc(fin, 1)
        nc.sync.sem_inc(fin, 1)
        nc.gpsimd.wait_ge(fin, 4)
        K.stats = (P.nins, P.nwaits)
    return nc, K


def _band_structure():
    BLOCK, N_BUCKETS, MAX_DIST, WINDOW = 128, 32, 128, 128
    qi = np.arange(BLOCK)[:, None]
    ki = np.arange(3 * BLOCK)[None, :]
    rel = ki - BLOCK - qi
    nb = N_BUCKETS // 2
    max_exact = nb // 2
    n = np.abs(rel)
    large = max_exact + (np.log(np.maximum(n, 1) / max_exact) / math.log(MAX_DIST / max_exact)
                         * (nb - max_exact)).astype(np.int32)
    large = np.minimum(large, nb - 1)
    buckets = (rel > 0).astype(np.int32) * nb + np.where(n < max_exact, n, large).astype(np.int32)
    band = n <= WINDOW
    return buckets, band


def core_consts(ctype, D):
    c = {}
    L = 16384 if ctype == 0 else 8192
    NN = 2 * L
    n = np.arange(NSLOT)
    pos = np.where(n < L, n, np.where((n > L) & (n < NN), NN - n, 0))
    selF = (n < L).astype(np.float32)
    selB = ((n > L) & (n < NN)).astype(np.float32)
    t = np.linspace(0.0, 1.0, L, dtype=np.float32)
    w = (np.float32(2.0 * math.pi) * np.arange(L, dtype=np.float32) / np.float32(L)).astype(np.float32)
    f = np.linspace(1e-4, 15, 16, dtype=np.float32)
    ang = (w[:, None] * f[None, :]).astype(np.float32)
    feats = np.concatenate([t[:, None], np.cos(ang), -np.sin(ang)], axis=-1).astype(np.float32)
    valid = (selF + selB)[:, None]
    c["featsT"] = np.ascontiguousarray((feats[pos] * valid).T.astype(np.float32))
    c["selfb"] = np.stack([selF, selB, t[pos] * (selF + selB)]).astype(np.float32)
    deltas = np.abs(np.linspace(math.log(1e-2) / 1.5, math.log(1e-2) / 0.3, D)).astype(np.float32)
    c["negdel"] = np.ascontiguousarray((-deltas).reshape(D // 128, 128).T)
    r = np.arange(128)
    n1 = np.arange(128)
    n2 = np.arange(256)
    if ctype == 0:
        N = 32768
        F1 = np.exp(-2j * np.pi * np.outer(n1, r) / 128.0)
        F1s, F1k = F1[:64], F1
        TW = np.exp(-2j * np.pi * np.outer(n2, r) / N)
        E = np.exp(2j * np.pi * np.outer(r, n1[:64]) / 128.0) / N
    else:
        N = 16384
        s_, k1 = r // 64, r % 64
        F1s = np.zeros((64, 128), complex)
        for a in range(64):
            m = a - 32 * s_
            ok = (m >= 0) & (m < 32)
            F1s[a] = np.where(ok, np.exp(-2j * np.pi * m * k1 / 64.0), 0)
        F1k = np.zeros((128, 128), complex)
        for a in range(64):
            F1k[a] = np.exp(-2j * np.pi * a * k1 / 64.0)
        TW = np.exp(-2j * np.pi * np.outer(n2, k1) / N)
        E = np.zeros((128, 64), complex)
        for a in range(64):
            m = a - 32 * s_
            ok = (m >= 0) & (m < 32)
            E[:, a] = np.where(ok, np.exp(2j * np.pi * m * k1 / 64.0) / N, 0)
    G = np.exp(-2j * np.pi * np.outer(n2, n2) / 256.0)
    H = np.exp(2j * np.pi * np.outer(n2, n2) / 256.0)
    c["c_f1s"] = np.concatenate([F1s.real, F1s.imag], 1).astype(np.float32)
    c["c_f1k"] = np.concatenate([F1k.real, F1k.imag], 1).astype(np.float32)
    tw = np.stack([TW.real, TW.imag], 1).reshape(2, 128, 2, 128)
    c["c_tw"] = np.ascontiguousarray(tw.transpose(1, 0, 2, 3)).astype(np.float32)
    g3 = np.stack([G.real, G.imag, -G.imag], 0).reshape(3, 2, 128, 256)
    c["c_g"] = np.ascontiguousarray(g3.transpose(2, 0, 1, 3)).astype(np.float32)
    HA = np.concatenate([H.real, H.imag], 1)
    HB = np.concatenate([-H.imag, H.real], 1)
    h2 = np.stack([HA, HB], 0).reshape(2, 2, 128, 512)
    c["c_h"] = np.ascontiguousarray(h2.transpose(2, 0, 1, 3)).astype(np.float32)
    TWC = np.conj(TW).T
    c["c_twc"] = np.ascontiguousarray(np.stack([TWC.real, TWC.imag], 1)).astype(np.float32)
    c["c_e"] = np.ascontiguousarray(np.stack([E.real, -E.imag], 1)).astype(np.float32)
    buckets, band = _band_structure()
    oh = np.zeros((32, 3, 128, 128), np.float32)
    for kb in range(3):
        bk = buckets[:, kb * 128:(kb + 1) * 128]
        for b in range(32):
            oh[b, kb] = (bk == b)
    c["c_oh"] = oh
    mk = np.where(band, 0.0, NEGM).astype(np.float32).reshape(128, 3, 128)
    c["c_mask"] = np.ascontiguousarray(mk.transpose(2, 1, 0))
    flag = 1.0 if ctype == 0 else 0.0
    c["c_seam"] = np.tile(np.array([[flag - 1.0, (flag - 1.0) * 30000.0]], np.float32), (128, 1))
    return c


def lay_w(W, oc_first=True):
    Kin, Nout = W.shape
    return np.ascontiguousarray(W.reshape(Kin // 128, 128, Nout // 128, 128).transpose(2, 1, 0, 3))


def vec_p(v):
    return np.ascontiguousarray(v.reshape(-1, 128).T)


def shared_inputs(cfg, inp):
    D, DFF, NH, NKV, DEPTH = cfg["D"], cfg["DFF"], cfg["NH"], cfg["NKV"], cfg["DEPTH"]
    NHY, NAT = (DEPTH + 1) // 2, DEPTH // 2
    f32 = np.float32
    m = {}
    g = np.concatenate([np.stack([vec_p(inp["norm_mix_g"][i]) for i in range(DEPTH)], 1).reshape(128, -1),
                        np.stack([vec_p(inp["norm_ffn_g"][i]) for i in range(DEPTH)], 1).reshape(128, -1)], 1)
    m["gains"] = np.ascontiguousarray(g, f32)
    m["w_in"] = np.stack([lay_w(inp["hy_w_in"][j]) for j in range(NHY)])
    m["w_out"] = np.stack([lay_w(inp["hy_w_out"][j]) for j in range(NHY)])
    nq, nk = NH * HD, NKV * HD
    if NAT:
        m["w_q"] = np.stack([lay_w(inp["at_w_qkv"][j][:, :nq]) for j in range(NAT)])
        m["w_k"] = np.stack([lay_w(inp["at_w_qkv"][j][:, nq:nq + nk]) for j in range(NAT)])
        m["w_v"] = np.stack([np.ascontiguousarray(inp["at_w_qkv"][j][:, nq + nk:].reshape(D // 128, 128, nk).transpose(1, 0, 2))
                             for j in range(NAT)])
        m["w_o"] = np.stack([lay_w(inp["at_w_o"][j]) for j in range(NAT)])
        m["atv"] = np.stack([np.concatenate([inp["at_q_g"][j][:, None], inp["at_k_g"][j][:, None],
                                             np.tile(inp["at_sink"][j][None, :], (128, 1))], 1) for j in range(NAT)]).astype(f32)
    gu = []
    for i in range(DEPTH):
        W = inp["ffn_w_gate_up"][i]
        a = lay_w(W[:, :DFF])
        b = lay_w(W[:, DFF:])
        gu.append(np.concatenate([a, b], -1))
    m["w_gu"] = np.stack(gu)
    m["w_dn"] = np.stack([lay_w(inp["ffn_w_down"][i]) for i in range(DEPTH)])
    hv = []
    for j in range(NHY):
        cw = inp["hy_conv_w"][j]
        hv.append(np.concatenate([vec_p(inp["hy_b_in"][j]), vec_p(cw[0]), vec_p(cw[1]), vec_p(cw[2]),
                                  vec_p(inp["hy_conv_b"][j]), vec_p(inp["hy_b_out"][j])], 1))
    m["hyv"] = np.stack(hv).astype(f32)
    m["skipr"] = np.stack([np.tile(inp["hy_skip"][j].reshape(1, -1), (64, 1)) for j in range(NHY)]).astype(f32)
    fm = []
    for j in range(NHY):
        w1p = np.zeros((64, 64), f32)
        w1p[:33] = inp["hy_f_w1"][j]
        fm.append(np.concatenate([inp["hy_f_w2"][j], inp["hy_f_w3"][j], w1p, inp["hy_f_b1"][j][:, None],
                                  inp["hy_f_b2"][j][:, None], inp["hy_f_b3"][j][:, None],
                                  inp["hy_f_freq"][j][:, None]], 1))
    m["fmlp"] = np.stack(fm).astype(f32)
    m["f_wout"] = np.ascontiguousarray(inp["hy_f_wout"], f32)
    if NAT:
        m["relb"] = np.ascontiguousarray(inp["rel_bias"], f32)
    return m


H3_DEPTH = dict(sig=2, ker=2, asb=4, tmp=8, y=4, kf=2, p=4, z=2, z1=2, gt=4)
FULL_CFG = dict(D=2048, DFF=5632, NH=16, NKV=4, DEPTH=4, FG=4, ilv=2, bal=4)
_CACHE = {}


def run_cores(cfg, inp, xs_list, ctypes, n_cores=8):
    D = cfg["D"]
    key = tuple(sorted((k, str(v)) for k, v in cfg.items()))
    if key not in _CACHE:
        _CACHE[key] = build(cfg)[0]
    nc = _CACHE[key]
    shared = shared_inputs(cfg, inp)
    consts = {ct: core_consts(ct, D) for ct in set(ctypes)}
    in_maps = []
    for i in range(n_cores):
        k = i if i < len(xs_list) else 0
        mp = dict(shared)
        mp.update(consts[ctypes[k]])
        mp["xT"] = np.ascontiguousarray(xs_list[k].T)
        in_maps.append(mp)
    res = run_bass_kernel_spmd(nc, in_maps, core_ids=list(range(n_cores)))
    if cfg.get("debug"):
        _CACHE["dbg"] = res.results
    return [np.ascontiguousarray(res.results[i]["yT"].T) for i in range(len(xs_list))]


def kernel(**inputs):
    inp = {k: np.asarray(v) for k, v in inputs.items()}
    xp, xsm = inp["x_prompt"], inp["x_sample"]
    xs_list = [xsm[0], np.concatenate([xp[0], xp[1]], 0), np.concatenate([xp[2], xp[3]], 0)]
    outs = run_cores(FULL_CFG, inp, xs_list, [0, 1, 1])
    y_sample = outs[0][None]
    y_prompt = np.stack([outs[1][:8192], outs[1][8192:], outs[2][:8192], outs[2][8192:]])
    return (y_prompt.astype(np.float32), y_sample.astype(np.float32))
```
